# Optimizing a Trainium2 kernel written in Bass

```python
import math
import jax, jax.numpy as jnp
from jax import lax
import numpy as np

D_MODEL = 2048
BATCH = 8
SEQ = 4096
DEPTH = 4

EPS = 1e-6
D_FF = 5504
D_MIX = D_MODEL // 2
N_BRANCH = 3
A_CONV = 3
B_HEADS = 8
B_HEAD_DIM = D_MIX // B_HEADS
B_CONV = 4
B_CHUNK = 64
C_GROUPS = 8
C_CHUNK = 128
N_ADA = 9
N_NORMS = 6
IN_WIDTHS = (D_MIX, D_MIX, D_MIX,
             3 * D_MIX, D_MIX, B_HEADS, B_HEADS,
             D_MIX, D_MIX,
             D_MODEL, D_MODEL, D_MODEL)
N_IN = sum(IN_WIDTHS)
SPLIT_POINTS = tuple(int(s) for s in np.cumsum(IN_WIDTHS)[:-1])

kernel_name = 'hybrid_gated_parallel_mixer_trunk'


def rmsnorm(x, g):
    xf = x.astype(jnp.float32)
    y = xf * lax.rsqrt(jnp.mean(xf * xf, axis=-1, keepdims=True) + EPS)
    return (y * g.astype(jnp.float32)).astype(x.dtype)


def layernorm(x, g, b):
    xf = x.astype(jnp.float32)
    mu = jnp.mean(xf, axis=-1, keepdims=True)
    xc = xf - mu
    y = xc * lax.rsqrt(jnp.mean(xc * xc, axis=-1, keepdims=True) + EPS)
    return (y * g.astype(jnp.float32) + b.astype(jnp.float32)).astype(x.dtype)


def l2norm(x):
    xf = x.astype(jnp.float32)
    return (xf * lax.rsqrt(jnp.sum(xf * xf, axis=-1, keepdims=True) + EPS)).astype(x.dtype)


def causal_dwconv(x, w):
    K, C = w.shape
    return lax.conv_general_dilated(x, w[:, None, :], window_strides=(1,), padding=[(K - 1, 0)],
                                    dimension_numbers=('NWC', 'WIO', 'NWC'), feature_group_count=C)


def swiglu(n, w_gate, w_up, w_down):
    return (jax.nn.silu(n @ w_gate) * (n @ w_up)) @ w_down


def short_conv_mixer(xa, ba, ca, w_conv):
    return ba * causal_dwconv(ca * xa, w_conv)


def chunked_delta_rule(q, k, v, g, beta):
    Bsz, S, H, dk = q.shape
    dv = v.shape[-1]
    N, C = S // B_CHUNK, B_CHUNK
    f32 = jnp.float32
    chunks = lambda t: t.astype(f32).reshape(Bsz, N, C, H, -1).transpose(1, 0, 3, 2, 4)
    qc, kc, vc = chunks(q), chunks(k), chunks(v)
    gc = g.astype(f32).reshape(Bsz, N, C, H).transpose(1, 0, 3, 2)
    bc = beta.astype(f32).reshape(Bsz, N, C, H).transpose(1, 0, 3, 2)
    gcum = jnp.cumsum(gc, axis=-1)
    tri_incl = jnp.tril(jnp.ones((C, C), bool))
    tri_strict = jnp.tril(jnp.ones((C, C), bool), -1)
    decay = jnp.exp(jnp.where(tri_incl, gcum[..., :, None] - gcum[..., None, :], -jnp.inf))
    kb = kc * bc[..., None]
    m = jnp.where(tri_strict, jnp.einsum('nbhid,nbhjd->nbhij', kb, kc) * decay, 0.0)
    eye = jnp.eye(C, dtype=f32)
    rhs = jnp.concatenate([vc * bc[..., None], kb * jnp.exp(gcum)[..., None]], axis=-1)
    sol = lax.linalg.triangular_solve(eye + m, rhs, left_side=True, lower=True, unit_diagonal=True)
    u, w = sol[..., :dv], sol[..., dv:]
    attn = jnp.where(tri_incl, jnp.einsum('nbhid,nbhjd->nbhij', qc, kc) * decay, 0.0)
    qg = qc * jnp.exp(gcum)[..., None]
    kd = kc * jnp.exp(gcum[..., -1:] - gcum)[..., None]
    glast = jnp.exp(gcum[..., -1])

    def step(state, inp):
        qg_c, kd_c, u_c, w_c, a_c, gl_c = inp
        v_new = u_c - jnp.einsum('bhcd,bhde->bhce', w_c, state)
        o = jnp.einsum('bhcd,bhde->bhce', qg_c, state) + jnp.einsum('bhij,bhje->bhie', a_c, v_new)
        state = state * gl_c[..., None, None] + jnp.einsum('bhcd,bhce->bhde', kd_c, v_new)
        return state, o

    s0 = jnp.zeros((Bsz, H, dk, dv), f32)
    _, o = lax.scan(step, s0, (qg, kd, u, w, attn, glast))
    return o.transpose(1, 0, 3, 2, 4).reshape(Bsz, S, H, dv).astype(q.dtype)


def gated_deltanet(qkv, z, beta_logit, a_logit, w_conv, a_log, dt_bias, norm_g):
    Bsz, S, _ = qkv.shape
    qkv = jax.nn.silu(causal_dwconv(qkv, w_conv))
    q, k, v = jnp.split(qkv, 3, axis=-1)
    heads = lambda t: t.reshape(Bsz, S, B_HEADS, B_HEAD_DIM)
    q = l2norm(heads(q)) * (B_HEAD_DIM ** -0.5)
    k = l2norm(heads(k))
    beta = jax.nn.sigmoid(beta_logit.astype(jnp.float32))
    g = -jnp.exp(a_log.astype(jnp.float32)) * jax.nn.softplus(a_logit.astype(jnp.float32) + dt_bias.astype(jnp.float32))
    o = chunked_delta_rule(q, k, heads(v), g, beta)
    o = rmsnorm(o, norm_g) * jax.nn.silu(heads(z))
    return o.reshape(Bsz, S, D_MIX)


def chunk_spatial_gating(u, v, ln_g, ln_b, w_s, b_s):
    Bsz, S, _ = u.shape
    N = S // C_CHUNK
    u = jax.nn.gelu(u)
    v = layernorm(jax.nn.gelu(v), ln_g, ln_b)
    vg = v.reshape(Bsz, N, C_CHUNK, C_GROUPS, D_MIX // C_GROUPS)
    w_causal = jnp.where(jnp.tril(jnp.ones((C_CHUNK, C_CHUNK), bool)), w_s, 0.0)
    mixed = jnp.einsum('gij,bnjgc->bnigc', w_causal, vg) + b_s.T[:, :, None]
    return u * mixed.reshape(Bsz, S, D_MIX)


def hybrid_token_mixing(n, w_in, conv_a, conv_qkv, a_log, dt_bias, gdn_norm_g,
                        ln_v_g, ln_v_b, w_spatial, b_spatial, w_branch, w_o):
    proj = n @ w_in
    (xa, ba, ca, qkv, z, beta_logit, a_logit, uc, vc,
     g_a, g_b, g_c) = jnp.split(proj, SPLIT_POINTS, axis=-1)
    y_a = short_conv_mixer(xa, ba, ca, conv_a)
    y_b = gated_deltanet(qkv, z, beta_logit, a_logit, conv_qkv, a_log, dt_bias, gdn_norm_g)
    y_c = chunk_spatial_gating(uc, vc, ln_v_g, ln_v_b, w_spatial, b_spatial)
    merged = (jax.nn.sigmoid(g_a) * (y_a @ w_branch[0])
              + jax.nn.sigmoid(g_b) * (y_b @ w_branch[1])
              + jax.nn.sigmoid(g_c) * (y_c @ w_branch[2]))
    return merged @ w_o


def setup_inputs(seed: int = 0) -> dict:
    key = jax.random.key(seed)
    ks = jax.random.split(key, 22)
    f32 = jnp.float32
    L = DEPTH
    nrm = lambda k, shape, s: jax.random.normal(k, shape, f32) * s
    dt = jnp.exp(jax.random.uniform(ks[11], (L, B_HEADS), f32, math.log(1e-3), math.log(1e-1)))
    return {
        'x': nrm(ks[0], (BATCH, SEQ, D_MODEL), 1.0),
        'c': nrm(ks[1], (BATCH, D_MODEL), 1.0),
        'w_ada': nrm(ks[2], (L, D_MODEL, N_ADA * D_MODEL), 0.5 * D_MODEL ** -0.5),
        'b_ada': nrm(ks[3], (L, N_ADA * D_MODEL), 0.01),
        'norm_g': 1.0 + nrm(ks[4], (L, N_NORMS, D_MODEL), 0.05),
        'ffn_w_gate': nrm(ks[5], (L, 2, D_MODEL, D_FF), D_MODEL ** -0.5),
        'ffn_w_up': nrm(ks[6], (L, 2, D_MODEL, D_FF), D_MODEL ** -0.5),
        'ffn_w_down': nrm(ks[7], (L, 2, D_FF, D_MODEL), D_FF ** -0.5),
        'w_in': nrm(ks[8], (L, D_MODEL, N_IN), D_MODEL ** -0.5),
        'conv_a': nrm(ks[9], (L, A_CONV, D_MIX), A_CONV ** -0.5),
        'conv_qkv': nrm(ks[10], (L, B_CONV, 3 * D_MIX), B_CONV ** -0.5),
        'a_log': jnp.log(jax.random.uniform(ks[12], (L, B_HEADS), f32, 1.0, 16.0)),
        'dt_bias': dt + jnp.log(-jnp.expm1(-dt)),
        'gdn_norm_g': 1.0 + nrm(ks[13], (L, B_HEAD_DIM), 0.05),
        'ln_v_g': 1.0 + nrm(ks[14], (L, D_MIX), 0.05),
        'ln_v_b': nrm(ks[15], (L, D_MIX), 0.01),
        'w_spatial': nrm(ks[16], (L, C_GROUPS, C_CHUNK, C_CHUNK), C_CHUNK ** -0.5),
        'b_spatial': 1.0 + nrm(ks[17], (L, C_GROUPS, C_CHUNK), 0.01),
        'w_branch': nrm(ks[18], (L, N_BRANCH, D_MIX, D_MODEL), D_MIX ** -0.5),
        'w_o': nrm(ks[19], (L, D_MODEL, D_MODEL), D_MODEL ** -0.5),
    }


def reference(x, c, w_ada, b_ada, norm_g, ffn_w_gate, ffn_w_up, ffn_w_down, w_in, conv_a, conv_qkv,
              a_log, dt_bias, gdn_norm_g, ln_v_g, ln_v_b, w_spatial, b_spatial, w_branch, w_o):
    c_act = jax.nn.silu(c)
    h = x
    for layer in range(DEPTH):
        ada = (c_act @ w_ada[layer] + b_ada[layer])[:, None, :]
        sh1, sc1, gt1, sh2, sc2, gt2, sh3, sc3, gt3 = jnp.split(ada, N_ADA, axis=-1)
        ng = norm_g[layer]
        n = rmsnorm(h, ng[0]) * (1.0 + sc1) + sh1
        f = swiglu(n, ffn_w_gate[layer, 0], ffn_w_up[layer, 0], ffn_w_down[layer, 0])
        h = h + 0.5 * gt1 * rmsnorm(f, ng[1])
        n = rmsnorm(h, ng[2]) * (1.0 + sc2) + sh2
        y = hybrid_token_mixing(n, w_in[layer], conv_a[layer], conv_qkv[layer], a_log[layer], dt_bias[layer],
                                gdn_norm_g[layer], ln_v_g[layer], ln_v_b[layer], w_spatial[layer],
                                b_spatial[layer], w_branch[layer], w_o[layer])
        h = h + gt2 * rmsnorm(y, ng[3])
        n = rmsnorm(h, ng[4]) * (1.0 + sc3) + sh3
        f = swiglu(n, ffn_w_gate[layer, 1], ffn_w_up[layer, 1], ffn_w_down[layer, 1])
        h = h + 0.5 * gt3 * rmsnorm(f, ng[5])
    return h
```

```python
import numpy as np
import concourse.bass as bass
import concourse.mybir as mybir
from concourse.bass_utils import run_bass_kernel_spmd
from contextlib import ExitStack

F32 = mybir.dt.float32
BF16 = mybir.dt.bfloat16
AF = mybir.ActivationFunctionType
ALU = mybir.AluOpType

D = 2048
SEQ = 4096
DFF = 5504
DMIX = 1024
NIN = 15376
NKC = 16
NHC = 43
T = 512
EPS = 1e-6
NSLOT = 4
PF = 2


class _Op:
    __slots__ = ("eng", "fn", "deps", "sig", "sem", "val", "dma")


class Sched:
    ENGS = ("pe", "act", "dve", "pool", "sp")

    def __init__(self, nc, es):
        self.nc = nc
        self.es = es
        self.ops = {e: [] for e in self.ENGS}
        self.last_w = {}
        self.readers = {}
        self.consts = set()
        self.eng_sem = {}
        self.dma_sems = {}
        self.dma_cnt = {}

    def add(self, eng, fn, reads=(), writes=(), dma_key=None):
        op = _Op()
        op.eng = eng
        op.fn = fn
        op.sig = False
        op.sem = None
        op.val = 0
        op.dma = dma_key is not None
        deps = []
        seen = set()
        for k in list(reads) + list(writes):
            w = self.last_w.get(k)
            if w is not None and id(w) not in seen:
                seen.add(id(w))
                deps.append(w)
        for k in writes:
            for r in self.readers.get(k, ()):
                if id(r) not in seen:
                    seen.add(id(r))
                    deps.append(r)
        for k in reads:
            if isinstance(k, tuple) and k[0] == "ps":
                for r in self.readers.get(k, ()):
                    if r.eng != eng and id(r) not in seen:
                        seen.add(id(r))
                        deps.append(r)
        op.deps = []
        for d in deps:
            if d.eng == eng and not d.dma and eng == "pe":
                continue
            op.deps.append(d)
            if not d.dma:
                d.sig = True
        for k in reads:
            if k in self.consts:
                continue
            self.readers.setdefault(k, []).append(op)
        for k in writes:
            self.last_w[k] = op
            self.readers[k] = []
        if op.dma:
            if dma_key not in self.dma_sems:
                self.dma_sems[dma_key] = self.es.enter_context(
                    self.nc.semaphore("d%d" % len(self.dma_sems)))
                self.dma_cnt[dma_key] = 0
            self.dma_cnt[dma_key] += 16
            op.sem = self.dma_sems[dma_key]
            op.val = self.dma_cnt[dma_key]
        self.ops[eng].append(op)
        return op

    def emit(self):
        nc = self.nc
        for e in self.ENGS:
            self.eng_sem[e] = self.es.enter_context(nc.semaphore("e_" + e))
        for e in self.ENGS:
            c = 0
            for op in self.ops[e]:
                if op.dma:
                    continue
                if op.sig:
                    c += 1
                    op.sem = self.eng_sem[e]
                    op.val = c
        block = self.es.enter_context(nc.Block())
        sched = self

        def run(eng_name, eng):
            waited = {}
            for op in sched.ops[eng_name]:
                for d in op.deps:
                    key = id(d.sem)
                    if waited.get(key, 0) >= d.val:
                        continue
                    waited[key] = d.val
                    eng.wait_ge(d.sem, d.val)
                if op.fn is None:
                    continue
                inst = op.fn(eng)
                if op.dma:
                    inst.then_inc(op.sem, 16)
                elif op.sig:
                    inst.then_inc(op.sem, 1)

        @block.tensor
        def _(e):
            run("pe", e)

        @block.scalar
        def _(e):
            run("act", e)

        @block.vector
        def _(e):
            run("dve", e)

        @block.gpsimd
        def _(e):
            run("pool", e)

        @block.sync
        def _(e):
            run("sp", e)


class Builder:
    def __init__(self, L, NT, do_mixer=True, do_ffn=True):
        self.L = L
        self.NT = NT
        self.do_mixer = do_mixer
        self.do_ffn = do_ffn
        self.use_scr = True
        self.nc = bass.Bass("TRN2", target_bir_lowering=False)
        self.es = ExitStack()

    def MM(self, out, lhsT, rhs, start, stop, r, w):
        self.S.add("pe", lambda e: e.matmul(out, lhsT=lhsT, rhs=rhs, start=start, stop=stop),
                   reads=r, writes=w)

    def TR(self, out, in_, r, w):
        ident = self.identf
        self.S.add("pe", lambda e: e.transpose(out, in_, ident[:]), reads=list(r) + ["identf"], writes=w)

    def ACT(self, out, in_, func, r, w, **kw):
        self.S.add("act", lambda e: e.activation(out=out, in_=in_, func=func, **kw), reads=r, writes=w)

    def TT(self, eng, out, in0, in1, op, r, w):
        self.S.add(eng, lambda e: e.tensor_tensor(out=out, in0=in0, in1=in1, op=op), reads=r, writes=w)

    def TS(self, eng, out, in0, s1, s2, op0, op1, r, w):
        if s2 is None:
            self.S.add(eng, lambda e: e.tensor_scalar(out=out, in0=in0, scalar1=s1, scalar2=None, op0=op0),
                       reads=r, writes=w)
        else:
            self.S.add(eng, lambda e: e.tensor_scalar(out=out, in0=in0, scalar1=s1, scalar2=s2, op0=op0, op1=op1),
                       reads=r, writes=w)

    def STT(self, eng, out, in0, scalar, in1, op0, op1, r, w):
        self.S.add(eng, lambda e: e.scalar_tensor_tensor(out=out, in0=in0, scalar=scalar, in1=in1, op0=op0, op1=op1),
                   reads=r, writes=w)

    def CP(self, eng, out, in_, r, w):
        self.S.add(eng, lambda e: e.tensor_copy(out=out, in_=in_), reads=r, writes=w)

    def RECIP(self, out, in_, r, w):
        self.S.add("dve", lambda e: e.reciprocal(out=out, in_=in_), reads=r, writes=w)

    def DMA(self, eng, out, in_, r, w, key):
        self.S.add(eng, lambda e: e.dma_start(out=out, in_=in_), reads=r, writes=w, dma_key=key)

    def sb(self, name, shape, dt):
        return self.es.enter_context(self.nc.sbuf_tensor(name, shape, dt))

    def bank(self):
        i = self.bank_i
        self.bank_i = (i + 1) % 8
        return self.pb[i], ("ps", i)

    def wplan_add(self, src2d, r0, nk, c0, ncols):
        ti, bi = self.plan_ti, self.plan_bi
        if ti is not None:
            self.plan_bi += 1
        self.wplan.append((src2d, r0, nk, c0, ncols, ti, bi))

    def _issue_load(self, idx):
        src2d, r0, nk, c0, ncols, ti, bi = self.wplan[idx]
        s = idx % NSLOT
        n = nk * ncols
        dst = self.wslot[s][:, 0:n].rearrange("p (k n) -> p k n", k=nk)
        if ti is None or ti == 0 or self.wscr is None:
            src = src2d[r0:r0 + nk * 128, c0:c0 + ncols].rearrange("(k p) n -> p k n", p=128)
            self.DMA("pool", dst, src, [], [("w", s)], ("w", s))
            if ti == 0 and self.wscr is not None and self.NT > 1:
                self.DMA("sp", self.wscr[bi // self.SCRB][bi % self.SCRB, :, 0:n], self.wslot[s][:, 0:n], [("w", s)], [("scr", bi)], ("wst", s))
        else:
            self.DMA("sp", self.wslot[s][:, 0:n], self.wscr[bi // self.SCRB][bi % self.SCRB, :, 0:n], [("scr", bi)], [("w", s)], ("w", s))

    def wnext(self, expect=None):
        idx = self.wi
        self.wi += 1
        while self.wissued < min(len(self.wplan), idx + PF + 1):
            self._issue_load(self.wissued)
            self.wissued += 1
        src2d, r0, nk, c0, ncols, _ti, _bi = self.wplan[idx]
        if expect is not None:
            assert expect == (r0, nk, c0, ncols), (expect, (r0, nk, c0, ncols))
        s = idx % NSLOT
        view = self.wslot[s][:, 0:nk * ncols].rearrange("p (k n) -> p k n", k=nk)
        return view, ("w", s)

    def build(self):
        nc, es = self.nc, self.es
        L, NT = self.L, self.NT
        dram = lambda name, shape, kind="ExternalInput": nc.dram_tensor(name, shape, F32, kind=kind).ap()
        x_d = dram("x", [SEQ, D])
        cT_d = dram("cT", [128, NKC])
        w_ada = dram("w_ada", [L, D, 9 * D])
        b_adaT = dram("b_adaT", [128, L, 144])
        ngT = dram("ngT", [128, L, 6, NKC])
        wg_d = dram("ffn_w_gate", [L, 2, D, DFF])
        wu_d = dram("ffn_w_up", [L, 2, D, DFF])
        wd_d = dram("ffn_w_down", [L, 2, DFF, D])
        out_d = dram("out", [SEQ, D], kind="ExternalOutput")
        self.nc_out = out_d
        self.drams = dict(x=x_d, cT=cT_d, w_ada=w_ada, b_adaT=b_adaT, ngT=ngT, wg=wg_d, wu=wu_d, wd=wd_d)
        if self.do_mixer:
            self.mixer_drams(dram)

        with es:
            self.S = S = Sched(nc, es)
            sb = self.sb
            self.pb = [es.enter_context(nc.psum_tensor("pb%d" % i, [128, 512], F32)) for i in range(8)]
            self.bank_i = 0
            self.stk = [("fsb", k) for k in range(NKC)]
            self.wslot = [sb("wslot%d" % i, [128, 4096], BF16) for i in range(NSLOT)]
            self.wplan = []
            self.wi = 0
            self.wissued = 0
            self.h = sb("h", [128, NKC, T], F32)
            self.nT = sb("nT", [128, NKC, T], BF16)
            self.stage = sb("stage", [128, 4, D], F32)
            self.fsb = self.stage[:].rearrange("p b d -> p (b d)").rearrange("p (k t) -> p k t", k=NKC)
            self.hid = sb("hid", [128, NHC, T], BF16)
            self.tmpf = sb("tmpf", [128, 2, T], F32)
            self.sq2 = sb("sq2", [128, 2, T], BF16)
            self.rstd = sb("rstd", [128, T], F32)
            self.identf = sb("identf", [128, 128], F32)
            self.onesb = sb("onesb", [128, 128], BF16)
            self.onesf = sb("onesf", [128, 128], F32)
            self.cT = sb("cTs", [128, NKC], F32)
            self.cTb = sb("cTb", [128, NKC], BF16)
            self.adaT = sb("adaT", [128, 4, 144], F32)
            self.ng = sb("ng", [128, L, 6, NKC], F32)
            self.gs = sb("gs", [128, 4, 3, NKC], F32)
            self.gg = sb("gg", [128, 4, 3, NKC], F32)
            S.consts.update(["identf", "onesb", "onesf", "adaT", "gs", "gg", "ng"])
            if self.do_mixer:
                self.mixer_alloc()

            self.plan_ti = None
            self.plan_bi = 0
            for l in range(L):
                for cb in range(9 * D // 256):
                    self.wplan_add(w_ada[l], 0, 16, cb * 256, 256)
            for ti in range(NT):
                self.plan_ti = ti
                self.plan_bi = 0
                for l in range(L):
                    if self.do_ffn:
                        self.plan_ffn(l, 0)
                    if self.do_mixer:
                        self.plan_mixer(l)
                    if self.do_ffn:
                        self.plan_ffn(l, 1)

            nblk = self.plan_bi
            SCRB = 240
            self.wscr = None
            if self.use_scr:
                self.wscr = [nc.dram_tensor("wscr%d" % i, [min(SCRB, nblk - i * SCRB), 128, 4096], BF16).ap()
                             for i in range((nblk + SCRB - 1) // SCRB)]
            self.SCRB = SCRB

            S.add("pool", lambda e: e.memset(self.identf[:], 1.0), writes=["identf"])
            S.add("pool", lambda e: e.affine_select(out=self.identf[:], in_=self.identf[:], pattern=[[-1, 128]],
                                                    compare_op=ALU.is_equal, fill=0.0, base=0,
                                                    channel_multiplier=1),
                  reads=["identf"], writes=["identf"])
            S.add("pool", lambda e: e.memset(self.onesb[:], 1.0), writes=["onesb"])
            S.add("pool", lambda e: e.memset(self.onesf[:], 1.0), writes=["onesf"])
            self.DMA("sp", self.cT[:], cT_d, [], ["cT"], "cT")
            self.DMA("sp", self.adaT[:, 0:L, :], b_adaT, [], ["adaT"], "badaT")
            self.DMA("sp", self.ng[:], ngT, [], ["ng"], "ng")
            if self.do_mixer:
                self.mixer_consts()
            self.ACT(self.cTb[:], self.cT[:], AF.Silu, ["cT"], ["cTb"])
            for l in range(L):
                pbk, pk = self.bank()
                for cb in range(72):
                    wv, wk = self.wnext()
                    for j in range(2):
                        col = cb * 2 + j
                        for kc in range(NKC):
                            self.MM(pbk[:, col:col + 1], wv[:, kc, j * 128:(j + 1) * 128], self.cTb[:, kc:kc + 1],
                                    kc == 0, kc == NKC - 1, [wk, "cTb"], [pk])
                self.TT("dve", self.adaT[:, l, :], pbk[:, 0:144], self.adaT[:, l, :], ALU.add,
                        [pk, "adaT"], ["adaT"])
                for i in range(3):
                    sc = self.adaT[:, l, (3 * i + 1) * 16:(3 * i + 2) * 16]
                    gt = self.adaT[:, l, (3 * i + 2) * 16:(3 * i + 3) * 16]
                    self.STT("dve", self.gs[:, l, i, :], sc, 1.0, self.ng[:, l, 2 * i, :], ALU.add, ALU.mult,
                             ["adaT", "ng"], ["gs"])
                    coef = 1.0 if i == 1 else 0.5
                    self.STT("dve", self.gg[:, l, i, :], gt, coef, self.ng[:, l, 2 * i + 1, :], ALU.mult, ALU.mult,
                             ["adaT", "ng"], ["gg"])

            for ti in range(NT):
                self.load_tile(ti)
                for l in range(L):
                    if self.do_ffn:
                        self.ffn(l, 0)
                    if self.do_mixer:
                        self.mixer(l, ti)
                    if self.do_ffn:
                        self.ffn(l, 1)
                self.store_tile(ti)
            S.add("sp", None, reads=["out"])
            assert getattr(self, 'stop', '') or self.wi == len(self.wplan), (self.wi, len(self.wplan))
            S.emit()
        return nc

    def load_tile(self, ti):
        x_d = self.drams["x"]
        src = x_d[ti * T:(ti + 1) * T, :].rearrange("(b p) d -> p b d", p=128)
        self.DMA("sp", self.stage[:], src, [], self.stk, "stage_in")
        for kc in range(NKC):
            pbk, pk = self.bank()
            for b in range(4):
                self.TR(pbk[:, b * 128:(b + 1) * 128], self.stage[:, b, kc * 128:(kc + 1) * 128], [("fsb", b * 4 + kc // 4)], [pk])
            eng = "dve" if kc % 2 == 0 else "act"
            if eng == "dve":
                self.CP("dve", self.h[:, kc, :], pbk[:], [pk], [("h", kc)])
            else:
                self.ACT(self.h[:, kc, :], pbk[:], AF.Copy, [pk], [("h", kc)])

    def store_tile(self, ti):
        for b in range(4):
            for q in range(4):
                pbk, pk = self.bank()
                for j in range(4):
                    kc = q * 4 + j
                    self.TR(pbk[:, j * 128:(j + 1) * 128], self.h[:, kc, b * 128:(b + 1) * 128], [("h", kc)], [pk])
                if (b * 4 + q) % 2 == 0:
                    self.CP("dve", self.stage[:, b, q * 512:(q + 1) * 512], pbk[:], [pk], [("fsb", b * 4 + q)])
                else:
                    self.ACT(self.stage[:, b, q * 512:(q + 1) * 512], pbk[:], AF.Copy, [pk], [("fsb", b * 4 + q)])
        dst = self.nc_out[ti * T:(ti + 1) * T, :].rearrange("(b p) d -> p b d", p=128)
        self.DMA("sp", dst, self.stage[:], self.stk, ["out"], "stage_out")

    def prenorm(self, l, i):
        pbk, pk = self.bank()
        for kc in range(NKC):
            s_ = kc % 2
            self.ACT(self.sq2[:, s_, :], self.h[:, kc, :], AF.Square, [("h", kc)], [("sq2", s_)])
            self.MM(pbk[:], self.onesb[:], self.sq2[:, s_, :], kc == 0, kc == NKC - 1, [("sq2", s_), "onesb"], [pk])
        self.ACT(self.rstd[:], pbk[:], AF.Ln, [pk], ["rstd"], scale=1.0 / D, bias=EPS)
        self.ACT(self.rstd[:], self.rstd[:], AF.Exp, ["rstd"], ["rstd"], scale=-0.5)
        sh = self.adaT[:, l, (3 * i) * 16:(3 * i + 1) * 16]
        for kc in range(NKC):
            tk = ("tmpf", kc % 2)
            self.TT("dve", self.tmpf[:, kc % 2, :], self.h[:, kc, :], self.rstd[:], ALU.mult,
                    [("h", kc), "rstd"], [tk])
            self.ACT(self.nT[:, kc, :], self.tmpf[:, kc % 2, :], AF.Identity, [tk, "gs", "adaT"], [("nT", kc)],
                     scale=self.gs[:, l, i, kc:kc + 1], bias=sh[:, kc:kc + 1])

    def postnorm(self, l, i):
        pbk, pk = self.bank()
        for kc in range(NKC):
            s_ = kc % 2
            self.ACT(self.sq2[:, s_, :], self.fsb[:, kc, :], AF.Square, [("fsb", kc)], [("sq2", s_)])
            self.MM(pbk[:], self.onesb[:], self.sq2[:, s_, :], kc == 0, kc == NKC - 1, [("sq2", s_), "onesb"], [pk])
        self.ACT(self.rstd[:], pbk[:], AF.Ln, [pk], ["rstd"], scale=1.0 / D, bias=EPS)
        self.ACT(self.rstd[:], self.rstd[:], AF.Exp, ["rstd"], ["rstd"], scale=-0.5)
        for kc in range(NKC):
            tk = ("tmpf", kc % 2)
            self.STT("dve", self.tmpf[:, kc % 2, :], self.fsb[:, kc, :], self.gg[:, l, i, kc:kc + 1], self.rstd[:],
                     ALU.mult, ALU.mult, [("fsb", kc), "gg", "rstd"], [tk])
            self.TT("dve", self.h[:, kc, :], self.h[:, kc, :], self.tmpf[:, kc % 2, :], ALU.add,
                    [("h", kc), tk], [("h", kc)])

    def plan_ffn(self, l, which):
        wg, wu, wd = self.drams["wg"], self.drams["wu"], self.drams["wd"]
        for hp in range(22):
            nc_ = 256 if hp < 21 else 128
            self.wplan_add(wg[l, which], 0, 16, hp * 256, nc_)
            self.wplan_add(wu[l, which], 0, 16, hp * 256, nc_)
        for dp in range(8):
            for kg in range(3):
                nk = 16 if kg < 2 else 11
                self.wplan_add(wd[l, which], kg * 2048, nk, dp * 256, 256)

    def ffn(self, l, which):
        i = 0 if which == 0 else 2
        self.prenorm(l, i)
        for hp in range(22):
            ncols = 256 if hp < 21 else 128
            gv, gk = self.wnext((0, 16, hp * 256, ncols))
            uv, uk = self.wnext((0, 16, hp * 256, ncols))
            for j in range(ncols // 128):
                hc = hp * 2 + j
                pg, pgk = self.bank()
                pu, puk = self.bank()
                for kc in range(NKC):
                    self.MM(pg[:], gv[:, kc, j * 128:(j + 1) * 128], self.nT[:, kc, :], kc == 0, kc == NKC - 1,
                            [gk, ("nT", kc)], [pgk])
                for kc in range(NKC):
                    self.MM(pu[:], uv[:, kc, j * 128:(j + 1) * 128], self.nT[:, kc, :], kc == 0, kc == NKC - 1,
                            [uk, ("nT", kc)], [puk])
                tk = ("tmpf", hc % 2)
                self.ACT(self.tmpf[:, hc % 2, :], pg[:], AF.Silu, [pgk], [tk])
                self.TT("dve", self.hid[:, hc, :], self.tmpf[:, hc % 2, :], pu[:], ALU.mult, [tk, puk], [("hid", hc)])
        for dp in range(8):
            pa, pak = self.bank()
            pbb, pbk = self.bank()
            pp = [(pa, pak), (pbb, pbk)]
            for kg in range(3):
                nk = 16 if kg < 2 else 11
                wv, wk = self.wnext((kg * 2048, nk, dp * 256, 256))
                for j in range(2):
                    for k in range(nk):
                        hc = kg * 16 + k
                        self.MM(pp[j][0][:], wv[:, k, j * 128:(j + 1) * 128], self.hid[:, hc, :], hc == 0, hc == NHC - 1,
                                [wk, ("hid", hc)], [pp[j][1]])
            for j in range(2):
                dc = dp * 2 + j
                if j == 0:
                    self.CP("dve", self.fsb[:, dc, :], pp[j][0][:], [pp[j][1]], [("fsb", dc)])
                else:
                    self.ACT(self.fsb[:, dc, :], pp[j][0][:], AF.Copy, [pp[j][1]], [("fsb", dc)])
        self.postnorm(l, i)


    def mixer_drams(self, dram):
        L = self.L
        d = self.drams
        d["w_in"] = dram("w_in", [L, D, NIN])
        d["w_branch"] = dram("w_branch", [L, 3, DMIX, D])
        d["w_o"] = dram("w_o", [L, D, D])
        d["conv_aT"] = dram("conv_aT", [128, L, 8, 3])
        d["conv_qT"] = dram("conv_qT", [128, L, 24, 4])
        d["gdn_gT"] = dram("gdn_gT", [128, L])
        d["lnv_gT"] = dram("lnv_gT", [128, L, 8])
        d["lnv_bT"] = dram("lnv_bT", [128, L, 8])
        d["wsT"] = dram("wsT", [L, 128, 8, 128])
        d["bsp_rep"] = dram("bsp_rep", [L, 128, 8, 128])
        d["alog_rep"] = dram("alog_rep", [128, L, 8])
        d["dtb_rep"] = dram("dtb_rep", [128, L, 8])

    def mixer_alloc(self):
        sb, L = self.sb, self.L
        self.Sst = sb("Sst", [128, L, 8, 128], F32)
        self.carA = sb("carA", [128, L, 8, 2], F32)
        self.carQ = sb("carQ", [128, L, 24, 3], F32)
        self.cwa = sb("cwa", [128, L, 8, 3], F32)
        self.cwq = sb("cwq", [128, L, 24, 4], F32)
        self.gdng = sb("gdng", [128, L], F32)
        self.lng = sb("lng", [128, L, 8], F32)
        self.lnb = sb("lnb", [128, L, 8], F32)
        self.alog = sb("alog", [128, L, 8], F32)
        self.dtb = sb("dtb", [128, L, 8], F32)
        self.negA = sb("negA", [128, L, 8], F32)
        self.maskSL = sb("maskSL", [128, 128], F32)
        self.maskIU = sb("maskIU", [128, 128], F32)
        self.maskU01 = sb("maskU01", [128, 128], F32)
        self.triT = sb("triT", [128, 128], F32)
        self.blk1 = sb("blk1", [128, 128], F32)
        self.cm = sb("cm", [128, 2], F32)
        self.wsf = sb("wsf", [128, 8, 128], F32)
        self.wsm = sb("wsm", [128, 8, 128], BF16)
        self.bspb = sb("bspb", [128, 8, 128], F32)
        self.gsm = sb("gsm", [128, 4, 10, 8], F32)
        self.glog = sb("glog", [128, 16], F32)
        self.vt2 = sb("vt2", [128, 2, 128], BF16)
        self.S.consts.update(["maskSL", "maskIU", "maskU01", "triT", "blk1", "cm", "cwa", "cwq", "gdng", "lng",
                              "lnb", "negA", "alog", "dtb"])
        hf = self.hid[:].rearrange("p c t -> p (c t)")
        v3 = lambda off, n: hf[:, off:off + n * T].rearrange("p (c t) -> p c t", c=n)
        self.yaT = v3(0, 8)
        self.ybT = v3(8 * T, 8)
        self.ycT = v3(16 * T, 8)
        self.mrg = v3(24 * T, 16)
        self.keysA = [("ya", c) for c in range(8)] + [("yb", c) for c in range(8)] + [("yc", c) for c in range(8)] \
            + [("mrg", c) for c in range(16)]
        self.sf = self.stage[:].rearrange("p b d -> p (b d)")
        TN = ["vb", "kbg", "kd0", "kd1", "dg", "expgc", "qg", "dec", "A0", "A1", "AT0", "AT1", "PT0", "PT1", "attnT",
              "negwT"]
        names = ["xp", "acc", "cas", "acc1", "mean", "rs", "gu", "t1", "t2", "xq", "rq", "macc1", "vnew"] \
            + ["gvT%d" % c for c in range(8)] + ["%s%d" % (n, s_) for n in ("qT", "kT", "vT") for s_ in range(2)] \
            + ["%s_%d" % (n, st) for n in TN for st in range(4)]
        self.ALLB = [("B", n) for n in names]
        self.keysB = []

    def Bv(self, off, n, name):
        k = ("B", name)
        assert k in self.ALLB, k
        return self.sf[:, off:off + n], k

    def fence(self, keys):
        d = self.rstd
        self.S.add("dve", lambda e: e.memset(self.glog[:, 0:1], 0.0), reads=[], writes=list(keys) + ["glog"])

    def mixer_consts(self):
        S, d = self.S, self.drams
        L = self.L
        self.DMA("sp", self.cwa[:], d["conv_aT"], [], ["cwa"], "c1")
        self.DMA("sp", self.cwq[:], d["conv_qT"], [], ["cwq"], "c2")
        self.DMA("sp", self.gdng[:], d["gdn_gT"], [], ["gdng"], "c3")
        self.DMA("sp", self.lng[:], d["lnv_gT"], [], ["lng"], "c4")
        self.DMA("sp", self.lnb[:], d["lnv_bT"], [], ["lnb"], "c5")
        self.DMA("sp", self.alog[:], d["alog_rep"], [], ["alog"], "c6")
        self.DMA("sp", self.dtb[:], d["dtb_rep"], [], ["dtb"], "c7")
        self.ACT(self.negA[:], self.alog[:], AF.Exp, ["alog"], ["negA"])
        self.TS("dve", self.negA[:], self.negA[:], -1.0, None, ALU.mult, None, ["negA"], ["negA"])
        S.add("pool", lambda e: e.memset(self.Sst[:], 0.0), writes=[("S", l, h) for l in range(L) for h in range(8)])
        S.add("pool", lambda e: e.memset(self.carA[:], 0.0), writes=["carA"])
        S.add("pool", lambda e: e.memset(self.carQ[:], 0.0), writes=["carQ"])

        def sel(t, key, pc, cm, base):
            S.add("pool", lambda e: e.affine_select(out=t[:], in_=t[:], pattern=[[pc, 128]], compare_op=ALU.is_ge,
                                                    fill=0.0, base=base, channel_multiplier=cm),
                  reads=[key], writes=[key])

        S.add("pool", lambda e: e.memset(self.maskU01[:], 1.0), writes=["maskU01"])
        sel(self.maskU01, "maskU01", 1, -1, 0)
        S.add("pool", lambda e: e.memset(self.blk1[:], 0.0), writes=["blk1"])
        S.add("pool", lambda e: e.memset(self.blk1[0:64, 0:64], 1.0), reads=["blk1"], writes=["blk1"])
        S.add("pool", lambda e: e.memset(self.blk1[64:128, 64:128], 1.0), reads=["blk1"], writes=["blk1"])
        self.TT("pool", self.triT[:], self.blk1[:], self.maskU01[:], ALU.mult, ["blk1", "maskU01"], ["triT"])
        self.TS("pool", self.maskIU[:], self.triT[:], -30000.0, 30000.0, ALU.mult, ALU.add, ["triT"], ["maskIU"])
        S.add("pool", lambda e: e.tensor_copy(out=self.maskSL[:], in_=self.blk1[:]), reads=["blk1"], writes=["maskSL"])
        sel(self.maskSL, "maskSL", -1, 1, -1)
        self.TS("pool", self.maskSL[:], self.maskSL[:], 30000.0, -30000.0, ALU.mult, ALU.add, ["maskSL"], ["maskSL"])
        S.add("pool", lambda e: e.memset(self.cm[:], 0.0), writes=["cm"])
        S.add("pool", lambda e: e.memset(self.cm[0:64, 0:1], 1.0), reads=["cm"], writes=["cm"])
        S.add("pool", lambda e: e.memset(self.cm[64:128, 1:2], 1.0), reads=["cm"], writes=["cm"])

    def plan_mixer(self, l):
        d = self.drams
        win, wbr, wo = d["w_in"][l], d["w_branch"], d["w_o"][l]
        add = self.wplan_add
        add(win, 0, 16, 7168, 16)
        for c2 in range(4):
            for base in (0, 2048, 1024):
                add(win, 0, 16, base + c2 * 256, 256)
        for c2 in range(4):
            add(win, 0, 16, 8208 + c2 * 256, 256)
        for c2 in range(4):
            add(win, 0, 16, 7184 + c2 * 256, 256)
        for h in range(8):
            for base in (3072, 4096, 5120, 6144):
                add(win, 0, 16, base + h * 128, 128)
        for d2 in range(8):
            for br in range(3):
                add(win, 0, 16, 9232 + br * 2048 + d2 * 256, 256)
                add(wbr[l, br], 0, 8, d2 * 256, 256)
        for d2 in range(8):
            add(wo, 0, 16, d2 * 256, 256)

    def proj(self, wv, wk, j, out_ps, pk):
        for kc in range(NKC):
            self.MM(out_ps, wv[:, kc, j * 128:(j + 1) * 128], self.nT[:, kc, :], kc == 0, kc == NKC - 1,
                    [wk, ("nT", kc)], [pk])

    def bcast_sumsq(self, src, srck, width, scale, out_r, outk):
        self.sqi = getattr(self, "sqi", 0) + 1
        s = self.sqi % 2
        self.ACT(self.sq2[:, s, 0:width], src, AF.Square, [srck], [("sq2", s)])
        pbk, pk = self.bank()
        self.MM(pbk[:, 0:width], self.onesb[:], self.sq2[:, s, 0:width], True, True, [("sq2", s), "onesb"], [pk])
        self.ACT(out_r, pbk[:, 0:width], AF.Ln, [pk], [outk], scale=scale, bias=EPS)
        self.ACT(out_r, out_r, AF.Exp, [outk], [outk], scale=-0.5)

    def gelu(self, dst, dstk, src_ps, pk, t1, t1k, t2, t2k):
        self.ACT(t1, src_ps, AF.Copy, [pk], [t1k])
        self.TT("dve", t2, t1, t1, ALU.mult, [t1k], [t2k])
        self.TT("dve", t2, t2, t1, ALU.mult, [t1k, t2k], [t2k])
        self.STT("dve", t2, t2, 0.044715, t1, ALU.mult, ALU.add, [t1k, t2k], [t2k])
        self.ACT(t2, t2, AF.Sigmoid, [t2k], [t2k], scale=1.5957691216057308)
        self.TT("dve", dst, t1, t2, ALU.mult, [t1k, t2k], [dstk])

    def mixer(self, l, ti):
        S, d = self.S, self.drams
        hidk = [("hid", c) for c in range(NHC)]
        if getattr(self, "stop", "") == "start":
            return
        self.fence(hidk + self.keysA + self.stk + self.ALLB)
        self.prenorm(l, 1)
        if getattr(self, "stop", "") == "pre":
            return
        tf0, tf1 = self.tmpf[:, 0, :], self.tmpf[:, 1, :]
        tk0, tk1 = ("tmpf", 0), ("tmpf", 1)
        self.DMA("sp", self.wsf[:], d["wsT"][l], [], ["wsf"], "wsf")
        self.DMA("sp", self.bspb[:], d["bsp_rep"][l], [], ["bspb"], "bspb")
        self.TT("dve", self.wsm[:], self.wsf[:], self.maskU01[:].unsqueeze(1).to_broadcast([128, 8, 128]), ALU.mult,
                ["wsf", "maskU01"], ["wsm"])

        if getattr(self, "stop", "") == "wsm":
            return
        wv, wk = self.wnext((0, 16, 7168, 16))
        G = self.gsm
        for b in range(4):
            pbk, pk = self.bank()
            for kc in range(NKC):
                self.MM(pbk[:, 0:16], self.nT[:, kc, b * 128:(b + 1) * 128], wv[:, kc, :], kc == 0, kc == NKC - 1,
                        [wk, ("nT", kc)], [pk])
            gk = ("gsm", b)
            self.ACT(G[:, b, 0, :], pbk[:, 0:8], AF.Sigmoid, [pk], [gk])
            self.TT("dve", G[:, b, 9, :], pbk[:, 8:16], self.dtb[:, l, :], ALU.add, [pk, "dtb"], [gk])
            self.ACT(G[:, b, 9, :], G[:, b, 9, :], AF.Exp, [gk], [gk])
            self.ACT(G[:, b, 9, :], G[:, b, 9, :], AF.Ln, [gk], [gk], bias=1.0)
            self.TT("dve", G[:, b, 1, :], G[:, b, 9, :], self.negA[:, l, :], ALU.mult, [gk, "negA"], [gk])
            p2, p2k = self.bank()
            self.MM(p2[:, 0:8], self.triT[:], G[:, b, 1, :], True, True, [gk, "triT"], [p2k])
            self.MM(p2[:, 8:16], self.blk1[:], G[:, b, 1, :], True, True, [gk, "blk1"], [p2k])
            self.CP("dve", G[:, b, 2, :], p2[:, 0:8], [p2k], [gk])
            self.TS("dve", G[:, b, 3, :], p2[:, 0:8], -1.0, None, ALU.mult, None, [p2k], [gk])
            self.CP("dve", G[:, b, 4, :], p2[:, 8:16], [p2k], [gk])
            self.ACT(G[:, b, 9, :], G[:, b, 2, :], AF.Exp, [gk], [gk])
            self.TT("dve", G[:, b, 5, :], G[:, b, 9, :], G[:, b, 0, :], ALU.mult, [gk], [gk])
            self.TT("dve", G[:, b, 9, :], G[:, b, 4, :], G[:, b, 2, :], ALU.subtract, [gk], [gk])
            self.ACT(G[:, b, 9, :], G[:, b, 9, :], AF.Exp, [gk], [gk])
            self.TS("dve", G[:, b, 6, :], G[:, b, 9, :], self.cm[:, 0:1], None, ALU.mult, None, [gk, "cm"], [gk])
            self.TS("dve", G[:, b, 7, :], G[:, b, 9, :], self.cm[:, 1:2], None, ALU.mult, None, [gk, "cm"], [gk])
            self.TS("dve", G[:, b, 8, :], G[:, b, 0, :], -1.0, None, ALU.mult, None, [gk], [gk])

        if getattr(self, "stop", "") == "gates":
            return
        xp, xpk = self.Bv(0, T + 2, "xp")
        acc, acck = self.Bv(520, T, "acc")
        cas, cask = self.Bv(1040, T, "cas")
        acc1, acc1k = self.Bv(1560, T, "acc1")
        accs = [(acc, acck), (acc1, acc1k)]
        for c2 in range(4):
            xv, xk = self.wnext((0, 16, c2 * 256, 256))
            cv, ck = self.wnext((0, 16, 2048 + c2 * 256, 256))
            for j in range(2):
                c = c2 * 2 + j
                ac, ack = accs[j]
                px, pxk = self.bank()
                pc, pck = self.bank()
                self.proj(xv, xk, j, px[:], pxk)
                self.proj(cv, ck, j, pc[:], pck)
                self.ACT(cas, pc[:], AF.Copy, [pck], [cask])
                self.CP("dve", xp[:, 0:2], self.carA[:, l, c, :], ["carA", ("carA", l, c)], [xpk])
                self.TT("dve", xp[:, 2:T + 2], cas, px[:], ALU.mult, [cask, pxk], [xpk])
                self.CP("dve", self.carA[:, l, c, :], xp[:, T:T + 2], [xpk], [("carA", l, c)])
                self.TS("dve", ac, xp[:, 0:T], self.cwa[:, l, c, 0:1], None, ALU.mult, None, [xpk, "cwa"], [ack])
                for k in (1, 2):
                    self.STT("dve", ac, xp[:, k:T + k], self.cwa[:, l, c, k:k + 1], ac, ALU.mult, ALU.add,
                             [xpk, "cwa", ack], [ack])
            bv, bk = self.wnext((0, 16, 1024 + c2 * 256, 256))
            for j in range(2):
                c = c2 * 2 + j
                ac, ack = accs[j]
                pbb, pbk = self.bank()
                self.proj(bv, bk, j, pbb[:], pbk)
                self.TT("dve", self.yaT[:, c, :], ac, pbb[:], ALU.mult, [ack, pbk], [("ya", c)])

        if getattr(self, "stop", "") == "A":
            return
        gvT = self.sf[:, 0:8 * T]
        gvT = gvT.rearrange("p (c t) -> p c t", c=8)
        gvk = lambda c: ("B", "gvT%d" % c)
        mean, meank = self.Bv(8 * T, T, "mean")
        rs, rsk = self.Bv(9 * T, T, "rs")
        gu, guk = self.Bv(10 * T, T, "gu")
        t1, t1k = self.Bv(11 * T, T, "t1")
        t2, t2k = self.Bv(12 * T, T, "t2")
        self.fence(self.ALLB)
        for c2 in range(4):
            vv, vk = self.wnext((0, 16, 8208 + c2 * 256, 256))
            for j in range(2):
                c = c2 * 2 + j
                pv, pvk = self.bank()
                self.proj(vv, vk, j, pv[:], pvk)
                self.gelu(gvT[:, c, :], gvk(c), pv[:], pvk, t1, t1k, t2, t2k)
        pm, pmk = self.bank()
        for c in range(8):
            self.MM(pm[:], self.onesf[:], gvT[:, c, :], c == 0, c == 7, [gvk(c), "onesf"], [pmk])
        pq, pqk = self.bank()
        for c in range(8):
            self.TT("dve", t1, gvT[:, c, :], gvT[:, c, :], ALU.mult, [gvk(c)], [t1k])
            self.MM(pq[:], self.onesf[:], t1, c == 0, c == 7, [t1k, "onesf"], [pqk])
        self.TS("dve", mean, pm[:], 1.0 / DMIX, None, ALU.mult, None, [pmk], [meank])
        self.TT("dve", t2, mean, mean, ALU.mult, [meank], [t2k])
        self.STT("dve", rs, pq[:], 1.0 / DMIX, t2, ALU.mult, ALU.subtract, [pqk, t2k], [rsk])
        self.ACT(rs, rs, AF.Ln, [rsk], [rsk], scale=1.0, bias=EPS)
        self.ACT(rs, rs, AF.Exp, [rsk], [rsk], scale=-0.5)
        for c in range(8):
            self.TT("dve", gvT[:, c, :], gvT[:, c, :], mean, ALU.subtract, [gvk(c), meank], [gvk(c)])
            self.TT("dve", gvT[:, c, :], gvT[:, c, :], rs, ALU.mult, [gvk(c), rsk], [gvk(c)])
            self.ACT(gvT[:, c, :], gvT[:, c, :], AF.Identity, [gvk(c), "lng", "lnb"], [gvk(c)],
                     scale=self.lng[:, l, c:c + 1], bias=self.lnb[:, l, c:c + 1])
        vi = 0
        for c2 in range(4):
            uv, uk = self.wnext((0, 16, 7184 + c2 * 256, 256))
            for j in range(2):
                g = c2 * 2 + j
                pu, puk = self.bank()
                self.proj(uv, uk, j, pu[:], puk)
                self.gelu(gu, guk, pu[:], puk, t1, t1k, t2, t2k)
                pmx, pmxk = self.bank()
                for b in range(4):
                    ptr, ptrk = self.bank()
                    self.TR(ptr[:, 0:128], gvT[:, g, b * 128:(b + 1) * 128], [gvk(g)], [ptrk])
                    s = vi % 2
                    vi += 1
                    self.CP("dve", self.vt2[:, s, :], ptr[:, 0:128], [ptrk], [("vt2", s)])
                    self.MM(pmx[:, b * 128:(b + 1) * 128], self.vt2[:, s, :], self.wsm[:, g, :], True, True,
                            [("vt2", s), "wsm"], [pmxk])
                self.TT("dve", t1.rearrange("p (b i) -> p b i", b=4), pmx[:].rearrange("p (b i) -> p b i", b=4),
                        self.bspb[:, g, :].unsqueeze(1).to_broadcast([128, 4, 128]), ALU.add, [pmxk, "bspb"], [t1k])
                self.TT("dve", self.ycT[:, g, :], t1, gu, ALU.mult, [t1k, guk], [("yc", g)])

        if getattr(self, "stop", "") == "C":
            return
        self.fence(self.ALLB + self.keysA + self.stk)
        G = self.gsm
        hf = self.hid[:].rearrange("p c t -> p (c t)")
        arA = hf[:, 24 * T:43 * T].bitcast(F32)
        qkv = []
        for s_ in range(2):
            row = []
            for qi, nm in enumerate(("qT", "kT", "vT")):
                v, k = self.Bv((s_ * 3 + qi) * T, T, "%s%d" % (nm, s_))
                row.append((v, k))
            qkv.append(row)
        xq, xqk = self.Bv(6 * T, T + 4, "xq")
        rq, rqk = arA[:, 4096:4096 + T], ("B", "rq")
        TN = ["vb", "kbg", "kd0", "kd1", "dg", "expgc", "qg", "dec", "A0", "A1", "AT0", "AT1", "PT0", "PT1", "attnT",
              "negwT"]
        base_b = 6 * T + 516
        sms = []
        for st in range(4):
            sm = {}
            for ti_, nm in enumerate(TN):
                k = ("B", "%s_%d" % (nm, st))
                assert k in self.ALLB
                if st < 2:
                    o_ = base_b + (st * 16 + ti_) * 128
                    sm[nm] = (self.sf[:, o_:o_ + 128], k)
                else:
                    o_ = ((st - 2) * 16 + ti_) * 128
                    sm[nm] = (arA[:, o_:o_ + 128], k)
            sms.append(sm)
        assert base_b + 2 * 16 * 128 + 128 <= 8192
        vnew, vnewk = self.Bv(base_b + 2 * 16 * 128, 128, "vnew")
        self.S.add("pool", lambda e: e.memset(vnew, 0.0), writes=[vnewk])
        zs = self.tmpf
        PO_BANK = 7

        def bank7():
            i = self.bank_i
            if i == PO_BANK:
                i = 0
            self.bank_i = (i + 1) % 8
            return self.pb[i], ("ps", i)

        def gen_projconv(h):
            s_ = h % 2
            for qi, base in ((0, 3072), (1, 4096), (2, 5120)):
                dst, dstk = qkv[s_][qi]
                wsl, wslk = self.wnext((0, 16, base + h * 128, 128))
                cc = qi * 8 + h
                pp, ppk = bank7()
                self.proj(wsl, wslk, 0, pp[:], ppk)
                self.CP("dve", xq[:, 0:3], self.carQ[:, l, cc, :], [("carQ", l, cc), "carQ"], [xqk])
                self.ACT(xq[:, 3:T + 3], pp[:], AF.Copy, [ppk], [xqk])
                self.CP("dve", self.carQ[:, l, cc, :], xq[:, T:T + 3], [xqk], [("carQ", l, cc)])
                yield
                self.TS("dve", rq, xq[:, 0:T], self.cwq[:, l, cc, 0:1], None, ALU.mult, None, [xqk, "cwq"], [rqk])
                for k in (1, 2, 3):
                    self.STT("dve", rq, xq[:, k:T + k], self.cwq[:, l, cc, k:k + 1], rq, ALU.mult, ALU.add,
                             [xqk, "cwq", rqk], [rqk])
                yield
                self.ACT(dst, rq, AF.Silu, [rqk], [dstk])
                if qi < 2:
                    self.sqi = getattr(self, "sqi", 0) + 1
                    s2 = self.sqi % 2
                    self.ACT(self.sq2[:, s2, :], dst, AF.Square, [dstk], [("sq2", s2)])
                    pbk, pk = bank7()
                    self.MM(pbk[:], self.onesb[:], self.sq2[:, s2, :], True, True, [("sq2", s2), "onesb"], [pk])
                    self.ACT(rq, pbk[:], AF.Ln, [pk], [rqk], scale=1.0, bias=EPS)
                    self.ACT(rq, rq, AF.Exp, [rqk], [rqk], scale=-0.5)
                    yield
                    if qi == 0:
                        self.STT("dve", dst, dst, 128.0 ** -0.5, rq, ALU.mult, ALU.mult, [dstk, rqk], [dstk])
                    else:
                        self.TT("dve", dst, dst, rq, ALU.mult, [dstk, rqk], [dstk])
                yield
            wz, wzk = self.wnext((0, 16, 6144 + h * 128, 128))
            pz, pzk = bank7()
            self.proj(wz, wzk, 0, pz[:], pzk)
            self.ACT(zs[:, s_, :], pz[:], AF.Silu, [pzk], [("tmpf", s_)])
            yield

        def gen_prep(h, b):
            sm = sms[b]
            s_ = h % 2
            (qT, qTk), (kT, kTk), (vT, vTk) = qkv[s_]
            bs = slice(b * 128, (b + 1) * 128)
            gk = ("gsm", b)
            gc = lambda q: G[:, b, q, h:h + 1]
            X = lambda nm: sm[nm][0]
            Kk = lambda nm: sm[nm][1]
            PE_ = "pool"
            self.TS(PE_, X("dg"), self.identf[:], gc(3), None, ALU.mult, None, ["identf", gk], [Kk("dg")])
            pt, ptk = bank7()
            self.TR(pt[:, 0:128], kT[:, bs], [kTk], [ptk])
            pv, pvk = bank7()
            self.TR(pv[:, 0:128], vT[:, bs], [vTk], [pvk])
            pg, pgk = bank7()
            self.MM(pg[:, 0:128], self.onesf[:], X("dg"), True, True, [Kk("dg"), "onesf"], [pgk])
            pq, pqk = bank7()
            self.MM(pq[:, 0:128], kT[:, bs], kT[:, bs], True, True, [kTk], [pqk])
            self.MM(pq[:, 128:256], kT[:, bs], qT[:, bs], True, True, [kTk, qTk], [pqk])
            self.ACT(X("vb"), pv[:, 0:128], AF.Copy, [pvk, gk], [Kk("vb")], scale=gc(0))
            self.TS("dve", X("kbg"), pt[:, 0:128], gc(5), None, ALU.mult, None, [ptk, gk], [Kk("kbg")])
            self.TS("dve", X("kd0"), pt[:, 0:128], gc(6), None, ALU.mult, None, [ptk, gk], [Kk("kd0")])
            self.TS("dve", X("kd1"), pt[:, 0:128], gc(7), None, ALU.mult, None, [ptk, gk], [Kk("kd1")])
            self.STT("dve", X("dec"), pg[:, 0:128], gc(2), self.maskSL[:], ALU.add, ALU.add, [pgk, gk, "maskSL"],
                     [Kk("dec")])
            self.STT("dve", X("negwT"), pg[:, 0:128], gc(2), self.maskIU[:], ALU.add, ALU.add,
                     [pgk, gk, "maskIU"], [Kk("negwT")])
            self.ACT(X("expgc"), pg[:, 0:128], AF.Exp, [pgk], [Kk("expgc")], scale=-1.0)
            self.TS("dve", X("A0"), pq[:, 0:128], gc(8), None, ALU.mult, None, [pqk, gk], [Kk("A0")])
            self.CP("dve", X("attnT"), pq[:, 128:256], [pqk], [Kk("attnT")])
            yield
            self.ACT(X("dec"), X("dec"), AF.Exp, [Kk("dec")], [Kk("dec")])
            self.ACT(X("negwT"), X("negwT"), AF.Exp, [Kk("negwT")], [Kk("negwT")], scale=-1.0)
            yield
            self.TT("dve", X("A0"), X("A0"), X("dec"), ALU.mult, [Kk("A0"), Kk("dec")], [Kk("A0")])
            self.TT("dve", X("attnT"), X("attnT"), X("negwT"), ALU.mult, [Kk("attnT"), Kk("negwT")], [Kk("attnT")])
            self.TT(PE_, X("qg"), qT[:, bs], X("expgc"), ALU.mult, [qTk, Kk("expgc")], [Kk("qg")])
            yield
            pn, pnk = bank7()
            self.TR(pn[:, 0:128], X("A0"), [Kk("A0")], [pnk])
            self.ACT(X("AT0"), pn[:, 0:128], AF.Copy, [pnk], [Kk("AT0")])
            self.TT("dve", X("PT0"), pn[:, 0:128], self.identf[:], ALU.add, [pnk, "identf"], [Kk("PT0")])
            yield
            An, ATn, Pn = ["A0", "A1"], ["AT0", "AT1"], ["PT0", "PT1"]
            pc = 0
            for j in range(0, 6):
                cu, nx = j % 2, (j + 1) % 2
                pa, pak = bank7()
                if j <= 4:
                    self.MM(pa[:, 0:128], X(ATn[cu]), X(An[cu]), True, True, [Kk(ATn[cu]), Kk(An[cu])], [pak])
                if j >= 1:
                    self.MM(pa[:, 128:256], X(An[cu]), X(Pn[pc]), True, True, [Kk(An[cu]), Kk(Pn[pc])], [pak])
                if j <= 3:
                    pat, patk = bank7()
                    self.MM(pat[:, 0:128], X(An[cu]), X(ATn[cu]), True, True, [Kk(ATn[cu]), Kk(An[cu])], [patk])
                if j <= 4:
                    self.CP("dve", X(An[nx]), pa[:, 0:128], [pak], [Kk(An[nx])])
                if j <= 3:
                    self.ACT(X(ATn[nx]), pat[:, 0:128], AF.Copy, [patk], [Kk(ATn[nx])])
                if j >= 1:
                    self.TT("dve", X(Pn[1 - pc]), pa[:, 128:256], X(Pn[pc]), ALU.add, [pak, Kk(Pn[pc])],
                            [Kk(Pn[1 - pc])])
                    pc = 1 - pc
                yield
            assert pc == 1
            pw, pwk = bank7()
            self.MM(pw[:, 0:128], X("kbg"), X("PT1"), True, True, [Kk("kbg"), Kk("PT1")], [pwk])
            self.ACT(X("negwT"), pw[:, 0:128], AF.Copy, [pwk], [Kk("negwT")], scale=-1.0)
            yield

        def gen_scan(h):
            s_ = h % 2
            Sh = self.Sst[:, l, h, :]
            Sk = ("S", l, h)
            po, pok = self.pb[PO_BANK], ("ps", PO_BANK)
            for b in range(4):
                sm = sms[b]
                X = lambda nm: sm[nm][0]
                Kk = lambda nm: sm[nm][1]
                bs = slice(b * 128, (b + 1) * 128)
                for c in range(2):
                    rs_ = slice(c * 64, (c + 1) * 64)
                    pvn, pvnk = bank7()
                    self.MM(pvn[:, 0:128], X("PT1"), X("vb"), True, False, [Kk("PT1"), Kk("vb")], [pvnk])
                    self.MM(pvn[:, 0:128], X("negwT"), Sh, False, True, [Kk("negwT"), Sk], [pvnk])
                    self.CP("dve", vnew[rs_, :], pvn[rs_, 0:128], [pvnk], [vnewk])
                    yield
                    ocol = slice(b * 128 + c * 64, b * 128 + (c + 1) * 64)
                    self.MM(po[:, ocol], Sh, X("qg")[:, rs_], True, False, [Sk, Kk("qg")], [pok])
                    self.MM(po[:, ocol], vnew, X("attnT")[:, rs_], False, True, [vnewk, Kk("attnT")], [pok])
                    psu, psuk = bank7()
                    kdn = "kd0" if c == 0 else "kd1"
                    self.MM(psu[:, 0:128], X(kdn), vnew, True, True, [Kk(kdn), vnewk], [psuk])
                    self.STT("dve", Sh, Sh, X("expgc")[:, c * 64 + 63:c * 64 + 64], psu[:, 0:128], ALU.mult, ALU.add,
                             [Sk, Kk("expgc"), psuk], [Sk])
                    yield
            self.sqi = getattr(self, "sqi", 0) + 1
            s2 = self.sqi % 2
            self.ACT(self.sq2[:, s2, :], po[:], AF.Square, [pok], [("sq2", s2)])
            pbk, pk = bank7()
            self.MM(pbk[:], self.onesb[:], self.sq2[:, s2, :], True, True, [("sq2", s2), "onesb"], [pk])
            r, rk = self.rstd[:], "rstd"
            self.ACT(r, pbk[:], AF.Ln, [pk], [rk], scale=1.0 / 128, bias=EPS)
            self.ACT(r, r, AF.Exp, [rk], [rk], scale=-0.5)
            self.TT("dve", r, po[:], r, ALU.mult, [pok, rk], [rk])
            self.STT("dve", self.ybT[:, h, :], r, self.gdng[:, l:l + 1], zs[:, s_, :], ALU.mult, ALU.mult,
                     [rk, "gdng", ("tmpf", s_)], [("yb", h)])
            yield

        def run_gens(gens):
            gens = list(gens)
            while gens:
                for g in list(gens):
                    try:
                        next(g)
                    except StopIteration:
                        gens.remove(g)

        sb_ = getattr(self, "stop", "")
        endf = lambda: self.fence(hidk + self.keysA + self.ALLB + self.stk)
        run_gens([gen_projconv(0)])
        if sb_ == "pc0":
            endf()
            return
        for h in range(8):
            run_gens([gen_prep(h, b) for b in range(4)])
            if sb_ == "prep0":
                endf()
                return
            run_gens([gen_scan(h)] + ([gen_projconv(h + 1)] if h < 7 else []))
            if sb_ == "h0":
                endf()
                return

        if getattr(self, "stop", "") == "B":
            return
        self.fence(self.ALLB + self.stk)
        macc = self.rstd
        macc1, macc1k = self.Bv(0, T, "macc1")
        maccs = [(macc[:], "rstd"), (macc1, macc1k)]
        ysrc = [(self.yaT, "ya"), (self.ybT, "yb"), (self.ycT, "yc")]
        for d2 in range(8):
            for br in range(3):
                gwv, gwk = self.wnext((0, 16, 9232 + br * 2048 + d2 * 256, 256))
                bwv, bwk = self.wnext((0, 8, d2 * 256, 256))
                for j in range(2):
                    dc = d2 * 2 + j
                    ma, mak = maccs[j]
                    pgt, pgtk = self.bank()
                    self.proj(gwv, gwk, j, pgt[:], pgtk)
                    ppr, pprk = self.bank()
                    for kc in range(8):
                        self.MM(ppr[:], bwv[:, kc, j * 128:(j + 1) * 128], ysrc[br][0][:, kc, :], kc == 0, kc == 7,
                                [bwk, (ysrc[br][1], kc)], [pprk])
                    tk = ("tmpf", j)
                    tf = self.tmpf[:, j, :]
                    self.ACT(tf, pgt[:], AF.Sigmoid, [pgtk], [tk])
                    if br == 0:
                        self.TT("dve", ma, tf, ppr[:], ALU.mult, [tk, pprk], [mak])
                    else:
                        self.TT("dve", tf, tf, ppr[:], ALU.mult, [tk, pprk], [tk])
                        if br == 1:
                            self.TT("dve", ma, ma, tf, ALU.add, [mak, tk], [mak])
                        else:
                            self.TT("dve", self.mrg[:, dc, :], ma, tf, ALU.add, [mak, tk], [("mrg", dc)])
        self.fence(self.ALLB + self.stk)
        for d2 in range(8):
            wv_, wk_ = self.wnext((0, 16, d2 * 256, 256))
            for j in range(2):
                dc = d2 * 2 + j
                py, pyk = self.bank()
                for kc in range(NKC):
                    self.MM(py[:], wv_[:, kc, j * 128:(j + 1) * 128], self.mrg[:, kc, :], kc == 0, kc == NKC - 1,
                            [wk_, ("mrg", kc)], [pyk])
                if j == 0:
                    self.CP("dve", self.fsb[:, dc, :], py[:], [pyk], [("fsb", dc)])
                else:
                    self.ACT(self.fsb[:, dc, :], py[:], AF.Copy, [pyk], [("fsb", dc)])
        self.postnorm(l, 1)
        self.fence(hidk + self.keysA + self.ALLB + self.stk)


_CACHE = {}


def _get_nc(L, NT, do_mixer, do_ffn):
    key = (L, NT, do_mixer, do_ffn)
    if key not in _CACHE:
        b = Builder(L, NT, do_mixer, do_ffn)
        _CACHE[key] = b.build()
    return _CACHE[key]


def make_in_maps(inputs, L=4, do_mixer=True):
    f = lambda k: np.asarray(inputs[k], dtype=np.float32)
    x = f("x")
    c = f("c")
    B = x.shape[0]
    b_adaT = np.ascontiguousarray(f("b_ada")[:L].reshape(L, 144, 128).transpose(2, 0, 1))
    ngT = np.ascontiguousarray(f("norm_g")[:L].reshape(L, 6, NKC, 128).transpose(3, 0, 1, 2))
    shared = {
        "w_ada": np.ascontiguousarray(f("w_ada")[:L]),
        "b_adaT": b_adaT,
        "ngT": ngT,
        "ffn_w_gate": np.ascontiguousarray(f("ffn_w_gate")[:L]),
        "ffn_w_up": np.ascontiguousarray(f("ffn_w_up")[:L]),
        "ffn_w_down": np.ascontiguousarray(f("ffn_w_down")[:L]),
    }
    if do_mixer:
        shared.update({
            "w_in": np.ascontiguousarray(f("w_in")[:L]),
            "w_branch": np.ascontiguousarray(f("w_branch")[:L]),
            "w_o": np.ascontiguousarray(f("w_o")[:L]),
            "conv_aT": np.ascontiguousarray(f("conv_a")[:L].reshape(L, 3, 8, 128).transpose(3, 0, 2, 1)),
            "conv_qT": np.ascontiguousarray(f("conv_qkv")[:L].reshape(L, 4, 24, 128).transpose(3, 0, 2, 1)),
            "gdn_gT": np.ascontiguousarray(f("gdn_norm_g")[:L].T),
            "lnv_gT": np.ascontiguousarray(f("ln_v_g")[:L].reshape(L, 8, 128).transpose(2, 0, 1)),
            "lnv_bT": np.ascontiguousarray(f("ln_v_b")[:L].reshape(L, 8, 128).transpose(2, 0, 1)),
            "wsT": np.ascontiguousarray(f("w_spatial")[:L].transpose(0, 3, 1, 2)),
            "bsp_rep": np.ascontiguousarray(np.broadcast_to(f("b_spatial")[:L][:, None], (L, 128, 8, 128))),
            "alog_rep": np.ascontiguousarray(np.broadcast_to(f("a_log")[:L][None], (128, L, 8))),
            "dtb_rep": np.ascontiguousarray(np.broadcast_to(f("dt_bias")[:L][None], (128, L, 8))),
        })
    maps = []
    for b in range(B):
        m = dict(shared)
        m["x"] = np.ascontiguousarray(x[b])
        m["cT"] = np.ascontiguousarray(c[b].reshape(NKC, 128).T)
        maps.append(m)
    return maps


def kernel(**inputs):
    nc = _get_nc(4, SEQ // T, True, True)
    maps = make_in_maps(inputs)
    res = run_bass_kernel_spmd(nc, maps, core_ids=list(range(8)))
    return np.stack([r["out"] for r in res.results], axis=0).astype(np.float32)
```
